# Optimizing a Trainium2 kernel written in Bass

```python
import math
import jax, jax.numpy as jnp
from jax import lax
import numpy as np

D_MODEL = 1024
BATCH = 4
SEQ = 8192
DEPTH = 1

N_MEM = 256
MIX_WIDTH = D_MODEL
CONV_WIDTH = MIX_WIDTH // 2
CONV_GROUPS = 8
CONV_K = 3
GLA_HEADS = 4
GLA_DV = (MIX_WIDTH - CONV_WIDTH) // GLA_HEADS
GLA_DK = GLA_DV // 2
GLA_K_TOTAL = GLA_HEADS * GLA_DK
GLA_V_TOTAL = GLA_HEADS * GLA_DV
GLA_LOWRANK = 16
GLA_GATE_NORM = 16.0
GLA_CHUNK = 64
XA_HEADS = 4
XA_HEAD_DIM = D_MODEL // XA_HEADS
D_FF = 4 * D_MODEL
EPS = 1e-6

SPLITS = [CONV_WIDTH, CONV_WIDTH, CONV_WIDTH,
          GLA_K_TOTAL, GLA_K_TOTAL, GLA_V_TOTAL, GLA_V_TOTAL,
          GLA_LOWRANK, GLA_LOWRANK]
W_IN_COLS = int(sum(SPLITS))

kernel_name = "hybrid_conv_gla_memxattn_encoder_block"


def rms_norm(x, g):
    xf = x.astype(jnp.float32)
    y = xf * lax.rsqrt(jnp.mean(xf * xf, axis=-1, keepdims=True) + EPS)
    return (y * g.astype(jnp.float32)).astype(x.dtype)


def split_cols(z):
    offs = np.cumsum(SPLITS)[:-1].tolist()
    return jnp.split(z, offs, axis=-1)


def short_conv_mixer(b_gate, c_gate, u, conv_w):
    h = c_gate * u
    hp = jnp.pad(h, ((0, 0), (1, 1), (0, 0)))
    y = (conv_w[0] * hp[:, :-2] + conv_w[1] * hp[:, 1:-1] + conv_w[2] * hp[:, 2:])
    return b_gate * y


def gla_chunked(q, k, v, log_a):
    bsz, s, h, dk = q.shape
    dv = v.shape[-1]
    nc = s // GLA_CHUNK

    def to_chunks(t):
        return t.reshape(bsz, nc, GLA_CHUNK, h, t.shape[-1]).transpose(0, 3, 1, 2, 4)

    q, k, v, la = to_chunks(q), to_chunks(k), to_chunks(v), to_chunks(log_a)
    b = jnp.cumsum(la, axis=3)
    q_t = q * jnp.exp(b)
    k_t = k * jnp.exp(-b)
    mask = jnp.tril(jnp.ones((GLA_CHUNK, GLA_CHUNK), dtype=bool))
    attn = jnp.einsum('bhncd,bhnsd->bhncs', q_t, k_t)
    attn = jnp.where(mask, attn, 0.0)
    o_intra = jnp.einsum('bhncs,bhnsv->bhncv', attn, v)

    g_tot = b[:, :, :, -1, :]
    k_hat = k * jnp.exp(g_tot[:, :, :, None, :] - b)
    u_chunk = jnp.einsum('bhncd,bhncv->bhndv', k_hat, v)

    def step(state, inp):
        g_n, u_n = inp
        new_state = jnp.exp(g_n)[..., None] * state + u_n
        return new_state, state

    s0 = jnp.zeros((bsz, h, dk, dv), jnp.float32)
    _, s_before = lax.scan(step, s0, (jnp.moveaxis(g_tot, 2, 0), jnp.moveaxis(u_chunk, 2, 0)))
    o_inter = jnp.einsum('bhncd,nbhdv->bhncv', q_t, s_before)
    o = o_intra + o_inter
    return o.transpose(0, 2, 3, 1, 4).reshape(bsz, s, h, dv)


def bidirectional_gla(q, k, v, la_f, la_b):
    fwd = gla_chunked(q, k, v, la_f)
    flip = lambda t: jnp.flip(t, axis=1)
    bwd = flip(gla_chunked(flip(q), flip(k), flip(v), flip(la_b)))
    diag = jnp.einsum('bshd,bshd->bsh', q, k)[..., None] * v
    return fwd + bwd - diag


def mixer_layer(x, mix_norm, w_in, conv_w, conv_norm, w_af, b_af, w_ab, b_ab, gla_norm, w_out):
    bsz, s, _ = x.shape
    h = rms_norm(x, mix_norm)
    z = h @ w_in
    cb, cc, cu, q, k, v, g, lr_f, lr_b = split_cols(z)

    y_conv = short_conv_mixer(cb, cc, cu, conv_w)
    y_conv = rms_norm(y_conv.reshape(bsz, s, CONV_GROUPS, CONV_WIDTH // CONV_GROUPS),
                      conv_norm.reshape(CONV_GROUPS, -1)).reshape(bsz, s, CONV_WIDTH)

    f32 = jnp.float32
    qh = q.astype(f32).reshape(bsz, s, GLA_HEADS, GLA_DK) * (GLA_DK ** -0.5)
    kh = k.astype(f32).reshape(bsz, s, GLA_HEADS, GLA_DK)
    vh = v.astype(f32).reshape(bsz, s, GLA_HEADS, GLA_DV)
    la_f = jax.nn.log_sigmoid((lr_f @ w_af + b_af).astype(f32)) / GLA_GATE_NORM
    la_b = jax.nn.log_sigmoid((lr_b @ w_ab + b_ab).astype(f32)) / GLA_GATE_NORM
    la_f = la_f.reshape(bsz, s, GLA_HEADS, GLA_DK)
    la_b = la_b.reshape(bsz, s, GLA_HEADS, GLA_DK)
    o = bidirectional_gla(qh, kh, vh, la_f, la_b)
    o = rms_norm(o, gla_norm).reshape(bsz, s, GLA_V_TOTAL).astype(x.dtype)
    y_gla = o * jax.nn.silu(g)

    y = jnp.concatenate([y_conv, y_gla], axis=-1)
    return y @ w_out


def memory_cross_attention(x, mem, xa_norm, mem_norm, w_xq, w_xkv, w_xo):
    bsz, s, _ = x.shape
    hq = rms_norm(x, xa_norm) @ w_xq
    kv = rms_norm(mem, mem_norm) @ w_xkv
    km, vm = jnp.split(kv, 2, axis=-1)
    qh = hq.reshape(bsz, s, XA_HEADS, XA_HEAD_DIM)
    kh = km.reshape(bsz, N_MEM, XA_HEADS, XA_HEAD_DIM)
    vh = vm.reshape(bsz, N_MEM, XA_HEADS, XA_HEAD_DIM)
    scores = jnp.einsum('bqhd,bmhd->bhqm', qh, kh).astype(jnp.float32) / math.sqrt(XA_HEAD_DIM)
    p = jax.nn.softmax(scores, axis=-1).astype(x.dtype)
    o = jnp.einsum('bhqm,bmhd->bqhd', p, vh).reshape(bsz, s, D_MODEL)
    return o @ w_xo


def sq_relu_mlp(x, mlp_norm, w_up, w_down):
    h = rms_norm(x, mlp_norm) @ w_up
    return jnp.square(jax.nn.relu(h)) @ w_down


def setup_inputs(seed: int = 0) -> dict:
    key = jax.random.key(seed)
    ks = jax.random.split(key, 24)
    L, D = DEPTH, D_MODEL
    nrm = lambda k, shape, fan_in: jax.random.normal(k, shape, jnp.float32) * (fan_in ** -0.5)
    gain = lambda k, shape: 1.0 + 0.01 * jax.random.normal(k, shape, jnp.float32)
    bias = lambda k, shape: 0.1 * jax.random.normal(k, shape, jnp.float32)
    return {
        "x": jax.random.normal(ks[0], (BATCH, SEQ, D), jnp.float32),
        "mem": jax.random.normal(ks[1], (BATCH, N_MEM, D), jnp.float32),
        "mix_norm": gain(ks[2], (L, D)),
        "w_in": nrm(ks[3], (L, D, W_IN_COLS), D),
        "conv_w": nrm(ks[4], (L, CONV_K, CONV_WIDTH), CONV_K),
        "conv_norm": gain(ks[5], (L, CONV_WIDTH)),
        "w_af": nrm(ks[6], (L, GLA_LOWRANK, GLA_K_TOTAL), GLA_LOWRANK),
        "b_af": bias(ks[7], (L, GLA_K_TOTAL)),
        "w_ab": nrm(ks[8], (L, GLA_LOWRANK, GLA_K_TOTAL), GLA_LOWRANK),
        "b_ab": bias(ks[9], (L, GLA_K_TOTAL)),
        "gla_norm": gain(ks[10], (L, GLA_DV)),
        "w_out": nrm(ks[11], (L, MIX_WIDTH, D), MIX_WIDTH),
        "xa_norm": gain(ks[12], (L, D)),
        "mem_norm": gain(ks[13], (L, D)),
        "w_xq": nrm(ks[14], (L, D, D), D),
        "w_xkv": nrm(ks[15], (L, D, 2 * D), D),
        "w_xo": nrm(ks[16], (L, D, D), D),
        "mlp_norm": gain(ks[17], (L, D)),
        "w_up": nrm(ks[18], (L, D, D_FF), D),
        "w_down": nrm(ks[19], (L, D_FF, D), D_FF),
        "final_norm": gain(ks[20], (D,)),
    }


def reference(x, mem, mix_norm, w_in, conv_w, conv_norm, w_af, b_af, w_ab, b_ab, gla_norm,
              w_out, xa_norm, mem_norm, w_xq, w_xkv, w_xo, mlp_norm, w_up, w_down, final_norm):
    for l in range(DEPTH):
        x = x + mixer_layer(x, mix_norm[l], w_in[l], conv_w[l], conv_norm[l], w_af[l], b_af[l],
                            w_ab[l], b_ab[l], gla_norm[l], w_out[l])
        x = x + memory_cross_attention(x, mem, xa_norm[l], mem_norm[l], w_xq[l], w_xkv[l], w_xo[l])
        x = x + sq_relu_mlp(x, mlp_norm[l], w_up[l], w_down[l])
    return rms_norm(x, final_norm)
```

```python
import numpy as np
import concourse.bass as bass
import concourse.mybir as mybir
from concourse.bass_utils import run_bass_kernel_spmd

F32 = mybir.dt.float32
BF16 = mybir.dt.bfloat16
AF = mybir.ActivationFunctionType
ALU = mybir.AluOpType
AX = mybir.AxisListType


class _Op:
    __slots__ = ("eng", "fn", "deps", "needs_inc", "inc_val", "dma_key", "dma_cum", "idx")


class Prog:
    STREAMS = ("sync", "act", "dve", "pool", "pe")

    def __init__(self, nc):
        self.nc = nc
        self.ops = []
        self.streams = {s: [] for s in self.STREAMS}
        self.last_writer = {}
        self.readers = {}
        self.dma_counts = {}
        self.dma_mode = {}

    def _add(self, eng, fn, reads, writes, dma_key=None, ndma=1, mode="slot"):
        op = _Op()
        op.eng, op.fn, op.needs_inc, op.inc_val = eng, fn, False, None
        op.dma_key = dma_key
        op.idx = len(self.ops)
        deps = set()
        for r in list(reads) + list(writes):
            w = self.last_writer.get(r)
            if w is not None:
                deps.add(w)
        for r in writes:
            for rd in self.readers.get(r, ()):
                deps.add(rd)
        deps.discard(op)
        op.deps = []
        for d in deps:
            if d.dma_key is None and d.eng == "pe" and eng == "pe":
                continue
            op.deps.append(d)
            d.needs_inc = True
        if dma_key is not None:
            self.dma_mode.setdefault(dma_key, mode)
            c = self.dma_counts.get(dma_key, 0) + 16 * ndma
            self.dma_counts[dma_key] = c
            op.dma_cum = c
        for r in reads:
            if r not in writes:
                self.readers.setdefault(r, []).append(op)
        for r in writes:
            self.last_writer[r] = op
            self.readers[r] = []
        self.ops.append(op)
        self.streams[eng].append(op)
        return op

    def op(self, eng, fn, reads=(), writes=()):
        return self._add(eng, fn, reads, writes)

    def dma(self, eng, fn, key, reads=(), writes=(), ndma=1, mode="slot"):
        return self._add(eng, fn, reads, writes, dma_key=key, ndma=ndma, mode=mode)

    def barrier(self, skip_keys=()):
        lasts = []
        for s in self.STREAMS:
            for op in reversed(self.streams[s]):
                if op.dma_key is None and op.fn is not None:
                    lasts.append(op)
                    break
        lastdma = {}
        for op in self.ops:
            if op.dma_key is not None and op.dma_key not in skip_keys:
                lastdma[op.dma_key] = op
        for s in self.STREAMS:
            op = _Op()
            op.eng, op.fn, op.needs_inc, op.inc_val, op.dma_key = s, None, False, None, None
            op.idx = len(self.ops)
            op.deps = [d for d in lasts if d.eng != s] + list(lastdma.values())
            for d in op.deps:
                d.needs_inc = True
            self.ops.append(op)
            self.streams[s].append(op)
        for r in list(self.readers.keys()):
            self.readers[r] = []

    def emit(self, block, semctx):
        nc = self.nc
        eng_sem = {}
        for s in ("act", "dve", "pool", "pe"):
            eng_sem[s] = semctx("e_" + s)
        dma_sem = {k: semctx("d_" + str(i)) for i, k in enumerate(sorted(self.dma_counts))}
        for s in self.STREAMS:
            c = 0
            for op in self.streams[s]:
                if op.dma_key is None and op.needs_inc:
                    c += 1
                    op.inc_val = c
        prog = self

        def run_stream(sname, e):
            waited = {}
            for op in prog.streams[sname]:
                for d in op.deps:
                    if d.dma_key is not None:
                        sem = dma_sem[d.dma_key]
                        val = prog.dma_counts[d.dma_key] if prog.dma_mode[d.dma_key] == "group" else d.dma_cum
                    else:
                        sem = eng_sem[d.eng]
                        val = d.inc_val
                    key = id(sem)
                    if waited.get(key, 0) >= val:
                        continue
                    waited[key] = val
                    e.wait_ge(sem, val)
                if op.fn is None:
                    continue
                r = op.fn(e)
                if op.dma_key is not None:
                    if not isinstance(r, (list, tuple)):
                        r = [r]
                    for ins in r:
                        ins.then_inc(dma_sem[op.dma_key], 16)
                elif op.needs_inc:
                    if isinstance(r, (list, tuple)):
                        r = r[-1]
                    r.then_inc(eng_sem[sname], 1)

        @block.sync
        def _(e):
            run_stream("sync", e)

        @block.scalar
        def _(e):
            run_stream("act", e)

        @block.vector
        def _(e):
            run_stream("dve", e)

        @block.gpsimd
        def _(e):
            run_stream("pool", e)

        @block.tensor
        def _(e):
            run_stream("pe", e)


NT = 32
SB_BASE = 16512
SB_END = 229344
EPS = 1e-6


class Alloc:
    def __init__(self, nc, base):
        self.nc, self.off = nc, base

    def __call__(self, name, shape, dt, at=None):
        n = 1
        for s in shape[1:]:
            n *= s
        nbytes = n * (4 if dt == F32 else 2)
        nbytes = (nbytes + 31) // 32 * 32
        off = self.off if at is None else at
        t = self.nc.alloc_sbuf_tensor_at(name, list(shape), dt, offset=off)
        if not hasattr(self, "offs"):
            self.offs = {}
        self.offs[name] = off
        if at is None:
            self.off += nbytes
        assert off + nbytes <= SB_END, (name, off, nbytes)
        return t


def build_program(debug=None):
    nc = bass.Bass("TRN2", target_bir_lowering=False)
    D = {}

    def din(name, shape):
        D[name] = nc.dram_tensor(name, list(shape), F32, kind="ExternalInput").ap()

    din("x_own", [NT * 128, 1024]); din("x_oth", [NT * 128, 1024]); din("mem", [256, 1024])
    din("w_in", [1024, 3104]); din("w_out", [1024, 1024]); din("w_xq", [1024, 1024])
    din("w_xkv", [1024, 2048]); din("w_xo", [1024, 1024]); din("w_up", [1024, 4096]); din("w_down", [4096, 1024])
    for g in ("g_mix", "g_xa", "g_mem", "g_mlp", "g_fin"):
        din(g, [1024])
    din("g_gla4", [512]); din("cw", [128, 12]); din("cnorm", [128, 4]); din("wa2", [33, 512]); din("wab1", [17, 256])
    din("ident", [128, 128]); din("tri4", [128, 512]); din("b64", [128, 128])
    out = nc.dram_tensor("out", [NT * 128, 1024], F32, kind="ExternalOutput").ap()
    x2s = nc.dram_tensor("x2s", [NT * 128, 1024], F32, kind="Internal").ap()
    dbg = None
    if debug is not None:
        dbg = nc.dram_tensor("dbg", list(debug), F32, kind="ExternalOutput").ap()

    P = Prog(nc)
    A0 = Alloc(nc, SB_BASE)
    ident = A0("ident", [128, 128], BF16)
    tri4 = A0("tri4", [128, 4, 128], F32)
    b64 = A0("b64", [128, 128], F32)
    cw = A0("cw", [128, 12], F32)
    cnorm = A0("cnorm", [128, 4], F32)
    wa2 = A0("wa2", [33, 512], BF16)
    wab1 = A0("wab1", [17, 256], BF16)
    lrT2 = A0("lrT2", [33, 128], BF16)
    lrT1 = A0("lrT1", [17, 128], BF16)
    lrT1_2 = A0("lrT1_2", [17, 128], BF16)
    lrT1_3 = A0("lrT1_3", [17, 128], BF16)
    onec = A0("onec", [128, 1], F32)
    gA = A0("gA", [128, 1024], F32)
    gB = A0("gB", [128, 1024], F32)
    stat = A0("stat", [128, 64], F32)
    junk = A0("junk", [128, 1024], BF16)
    RBASE = A0.off

    import contextlib
    es = contextlib.ExitStack()
    tbanks = [es.enter_context(nc.psum_tensor("tpb%d" % i, [128, 1024], BF16)) for i in range(2)]
    mbanks = [es.enter_context(nc.psum_tensor("mb%d" % i, [128, 512], F32)) for i in range(6)]
    tctr = [0]
    mctr = [0]

    def tbank():
        i = tctr[0] % 2
        tctr[0] += 1
        return tbanks[i], "tpb%d" % i

    def mbank():
        i = mctr[0] % 6
        mctr[0] += 1
        return mbanks[i], "mb%d" % i

    def dma(eng, out_ap, in_ap, key, reads=(), writes=(), mode="slot"):
        P.dma(eng, lambda e: [e.dma_start(out=out_ap, in_=in_ap)], key, reads=reads, writes=writes, mode=mode)

    def dmas(eng, pairs, key, reads=(), writes=(), mode="slot"):
        P.dma(eng, lambda e: [e.dma_start(out=o, in_=i) for (o, i) in pairs], key, reads=reads, writes=writes,
              ndma=len(pairs), mode=mode)

    def mm(items, reads, writes):
        def fn(e):
            r = None
            for (o, l, rh, st, sp) in items:
                r = e.matmul(o, lhsT=l, rhs=rh, start=st, stop=sp)
            return r
        P.op("pe", fn, reads, writes)

    def tr(items, reads, writes):
        def fn(e):
            r = None
            for (o, i) in items:
                r = e.transpose(out=o, in_=i, identity=ident[:])
            return r
        P.op("pe", fn, list(reads) + ["ident"], writes)

    def act(out_ap, in_ap, func, reads, writes, bias=None, scale=None, accum=None):
        kw = {}
        if bias is not None:
            kw["bias"] = bias
        if scale is not None:
            kw["scale"] = scale
        if accum is not None:
            kw["accum_out"] = accum
        P.op("act", lambda e: e.activation(out=out_ap, in_=in_ap, func=func, **kw), reads, writes)

    def acts(lst, reads, writes):
        def fn(e):
            r = None
            for (o, i, f, kw) in lst:
                r = e.activation(out=o, in_=i, func=f, **kw)
            return r
        P.op("act", fn, reads, writes)

    def cp(eng, out_ap, in_ap, reads, writes):
        if eng == "act":
            P.op("act", lambda e: e.activation(out=out_ap, in_=in_ap, func=AF.Copy), reads, writes)
        else:
            P.op(eng, lambda e: e.tensor_copy(out=out_ap, in_=in_ap), reads, writes)

    def tt(eng, out_ap, in0, in1, op, reads, writes):
        P.op(eng, lambda e: e.tensor_tensor(out=out_ap, in0=in0, in1=in1, op=op), reads, writes)

    def stt(eng, out_ap, in0, scalar, in1, op0, op1, reads, writes):
        P.op(eng, lambda e: e.scalar_tensor_tensor(out=out_ap, in0=in0, scalar=scalar, in1=in1, op0=op0, op1=op1),
             reads, writes)

    def ts(eng, out_ap, in0, s1, s2, op0, op1, reads, writes):
        if s2 is None:
            P.op(eng, lambda e: e.tensor_scalar(out=out_ap, in0=in0, scalar1=s1, scalar2=None, op0=op0), reads, writes)
        else:
            P.op(eng, lambda e: e.tensor_scalar(out=out_ap, in0=in0, scalar1=s1, scalar2=s2, op0=op0, op1=op1),
                 reads, writes)

    def multi(eng, fns, reads, writes):
        def fn(e):
            r = None
            for f in fns:
                r = f(e)
            return r
        P.op(eng, fn, reads, writes)

    statctr = [0]

    def rms_stats(x_ap, xres, width_scale):
        k = statctr[0] % 16
        statctr[0] += 1
        rn = "stat%d" % k
        c = stat[:, 4 * k:4 * k + 1]
        act(junk[:, 0:x_ap.shape[-1]], x_ap, AF.Square, [xres], ["junk", rn], scale=width_scale, accum=c)
        act(c, c, AF.Ln, [], [rn], bias=EPS, scale=1.0)
        act(c, c, AF.Exp, [], [rn], scale=-0.5)
        return c, rn

    def norm_to_hT(x_ap, xres, gain, gres, hn_t, hnres, dst_ap, dstres, copy_eng="act"):
        c, rn = rms_stats(x_ap, xres, 1.0 / 32.0)
        stt("dve", hn_t[:], x_ap, c, gain[:], ALU.mult, ALU.mult, [xres, rn, gres], [hnres])
        tb, tres = tbank()
        tr([(tb[:, kc * 128:(kc + 1) * 128], hn_t[:, kc * 128:(kc + 1) * 128]) for kc in range(8)], [hnres], [tres])
        cp(copy_eng, dst_ap, tb[:].rearrange("p (k t) -> p k t", k=8), [tres], [dstres])

    def norm_part(x_ap, xres, gain, gres, hn_t, hnres):
        c, rn = rms_stats(x_ap, xres, 1.0 / 32.0)
        stt("dve", hn_t[:], x_ap, c, gain[:], ALU.mult, ALU.mult, [xres, rn, gres], hnres)

    def tr_part(hn_t, hnres, dst_ap, dstres, copy_eng="act"):
        tb, tres = tbank()
        tr([(tb[:, kc * 128:(kc + 1) * 128], hn_t[:, kc * 128:(kc + 1) * 128]) for kc in range(8)], hnres, [tres])
        cp(copy_eng, dst_ap, tb[:].rearrange("p (k t) -> p k t", k=8), [tres], dstres)

    AM = Alloc(nc, RBASE)
    w_in = AM("w_in", [128, 8, 3104], BF16)
    w_out = AM("w_out", [128, 8, 1024], BF16)
    xq_off = AM.off
    w_xq = AM("w_xq", [128, 8, 1024], BF16)
    w_xo = AM("w_xo", [128, 8, 1024], BF16)
    w_xkv = AM("w_xkv", [128, 8, 2048], BF16, at=xq_off)
    KmT = AM("KmT", [128, 8, 256], BF16)
    Vm = AM("Vm", [128, 2, 1024], BF16)
    Sstore = AM("Sstore", [128, NT, 256], BF16)
    ggla = AM("ggla", [128, 512], F32)
    xres = [AM("xres%d" % i, [128, 1024], F32) for i in range(3)]
    hn = [AM("hn%d" % i, [128, 1024], BF16) for i in range(2)]
    hT = [AM("hT%d" % i, [128, 8, 130], BF16) for i in range(3)]
    hhalo = AM("hhalo", [128, 8, 1], BF16)
    Spb = AM("Spb", [128, 256], F32)
    Spf = AM("Spf", [128, 256], F32)
    Spf_bf = [AM("Spfb%d" % i, [128, 256], BF16) for i in range(2)]
    v_sb = AM("v_sb", [128, 512], BF16)
    sp = AM("sp", [128, 512], F32)
    Eneg = AM("Eneg", [128, 512], F32)
    Epos = AM("Epos", [128, 512], F32)
    ECtm = AM("ECtm", [128, 512], F32)
    khat = AM("khat", [128, 512], BF16)
    qk4 = AM("qk4", [128, 4, 4, 128], BF16)
    dec = AM("dec", [128, 6], F32)
    h_sb = AM("h_sb", [128, 4, 130], F32)
    y_sb = AM("y_sb", [128, 4, 128], F32)
    ysq = AM("ysq", [128, 4, 128], F32)
    rstdc = AM("rstdc", [128, 512], F32)
    t1 = AM("t1", [128, 1024], F32)
    AT = AM("AT", [128, 512], BF16)
    eg = AM("eg", [128, 512], F32)
    gs = eg
    yg = AM("yg", [128, 512], BF16)
    yT = AM("yT", [128, 8, 128], BF16)
    hT2 = AM("hT2", [128, 8, 128], BF16)
    qxT = AM("qxT", [128, 8, 128], BF16)
    Pn = AM("Pn", [128, 1024], BF16)
    PT = AM("PT", [128, 8, 128], BF16)
    oT = AM("oT", [128, 8, 128], BF16)
    sm4 = AM("sm4", [128, 16], F32)
    cta = AM("cta", [128, 128], F32)
    ctb = AM("ctb", [128, 128], F32)
    pre_hT5 = AM("pre_hT5", [128, 8, 128], BF16, at=AM.offs["Eneg"])
    pre_hn2 = AM("pre_hn2", [128, 1024], BF16, at=AM.offs["Epos"])
    pre_hn = [(hn[0], ["hn0"]), (hn[1], ["hn1"]), (pre_hn2, ["Epos"])]
    pre_hT1 = AM("pre_hT1", [128, 8, 128], BF16, at=AM.offs["y_sb"])
    pre_hT = [(yT, ["yTc", "yTg"]), (pre_hT1, ["y0", "y1", "y2", "y3"]), (qxT, ["qxT0", "qxT1"]), (PT, ["PT"]), (oT, ["oT0", "oT1"]),
              (pre_hT5, ["Eneg"])]
    print("phase M sbuf end", AM.off, "limit", SB_END)

    def wload(dst, src, rows_per=128, key=None, res=None, col0=0, col1=None, nk=8):
        col1 = src.shape[1] if col1 is None else col1
        pairs = [(dst[:, k, col0:col1], src[k * 128:(k + 1) * 128, col0:col1]) for k in range(nk)]
        dmas("pool", pairs, key, writes=(res if isinstance(res, list) else [res]), mode="group")

    dma("pool", ident[:], D["ident"][:, :], "c_ident", writes=["ident"], mode="group")
    dma("pool", wab1[:], D["wab1"][:, :], "c_wab1", writes=["wab1"], mode="group")
    dma("pool", wa2[:], D["wa2"][:, :], "c_wa2", writes=["wa2"], mode="group")
    dmas("sync", [(tri4[:].rearrange("p a t -> p (a t)"), D["tri4"][:, :]), (b64[:], D["b64"][:, :]),
                  (cw[:], D["cw"][:, :]), (cnorm[:], D["cnorm"][:, :])], "c_misc", writes=["tri4", "b64", "cw", "cnorm"],
         mode="group")
    dma("sync", gA[:], D["g_mix"].partition_broadcast(128), "c_gA", writes=["gA"])
    dma("sync", gB[:], D["g_mem"].partition_broadcast(128), "c_gB", writes=["gB"])
    dma("sync", ggla[:], D["g_gla4"].partition_broadcast(128), "c_ggla", writes=["ggla"], mode="group")
    wload(w_in, D["w_in"], key="w_kv", res="w_in_kv", col0=1792, col1=2560)
    wload(w_in, D["w_in"], key="w_lr", res="w_in_lr", col0=3072, col1=3104)
    wload(w_xkv, D["w_xkv"], key="w_xkv", res=["w_xq", "w_xo"])
    P.op("dve", lambda e: e.memset(lrT2[:], 1.0), [], ["lrT2"])
    P.op("dve", lambda e: e.memset(lrT1[:], 1.0), [], ["lrT1_0"])
    P.op("dve", lambda e: e.memset(lrT1_2[:], 1.0), [], ["lrT1_1"])
    P.op("dve", lambda e: e.memset(lrT1_3[:], 1.0), [], ["lrT1_2"])
    P.op("dve", lambda e: e.memset(onec[:], 1.0), [], ["onec"])
    P.op("dve", lambda e: e.memset(Spb[:], 0.0), [], ["Spb"])
    P.op("dve", lambda e: e.memset(Spf[:], 0.0), [], ["Spf"])
    P.op("dve", lambda e: e.memset(Spf_bf[0][:], 0.0), [], ["Spfb0"])
    P.op("dve", lambda e: e.memset(hT[0][:, :, 0:1], 0.0), [], ["hT0"])
    P.op("pool", lambda e: e.memset(qk4[:], 0.0), [], ["qt_pf", "qt_pb", "kt_pf", "kt_pb"])

    wload(w_in, D["w_in"], key="w_a", res="w_in_a", col0=0, col1=1792)
    wload(w_in, D["w_in"], key="w_g", res="w_in_g", col0=2560, col1=3072)
    wload(w_out, D["w_out"], key="w_out", res="w_out")

    def kv_compute():
        for mt in range(2):
            s = mt % 2
            dma("sync", t1[:], D["mem"][mt * 128:(mt + 1) * 128, :], "kvx", writes=["t1a", "t1b"])
            norm_to_hT(t1[:], "t1a", gB, "gB", hn[s], "hn%d" % s, hT2[:], "hT2")
            for half in range(2):
                pb, pres = mbank()
                mm([(pb[:, c * 128:(c + 1) * 128], w_xkv[:, kc, (half * 4 + c) * 128:(half * 4 + c + 1) * 128], hT2[:, kc, :],
                     kc == 0, kc == 7) for c in range(4) for kc in range(8)], ["w_xq", "w_xo", "hT2"], [pres])
                cp("act", KmT[:, half * 4:half * 4 + 4, mt * 128:(mt + 1) * 128], pb[:].rearrange("p (c t) -> p c t", c=4),
                   [pres], ["KmT"])
            for half in range(2):
                pb, pres = mbank()
                mm([(pb[:], hT2[:, kc, :], w_xkv[:, kc, 1024 + half * 512:1024 + (half + 1) * 512], kc == 0, kc == 7)
                    for kc in range(8)], ["w_xq", "w_xo", "hT2"], [pres])
                cp("act", Vm[:, mt, half * 512:(half + 1) * 512], pb[:], [pres], ["Vm"])
        wload(w_xq, D["w_xq"], key="w_xq2", res="w_xq")
        wload(w_xo, D["w_xo"], key="w_xo2", res="w_xo")
        dma("sync", gB[:], D["g_xa"].partition_broadcast(128), "c_gB", reads=[], writes=["gB"])

    WIN = ["w_in_kv", "w_in_lr", "w_in_a", "w_in_g"]

    lrT1b = [lrT1, lrT1_2, lrT1_3]

    def pre_load(j):
        s3 = j % 3
        src = D["x_oth"][(j - NT) * 128:(j - NT + 1) * 128, :] if j >= NT else D["x_own"][j * 128:(j + 1) * 128, :]
        dma("sync", xres[s3][:], src, "xr%d" % s3, writes=["xres%d" % s3])

    def prepass_front(j):
        s3 = j % 3
        xr = "xres%d" % s3
        hnb, hnr = pre_hn[s3]
        if j >= 2 * NT - 3:
            pre_load(j)
        yield
        norm_part(xres[s3][:], xr, gA, "gA", hnb, hnr)
        if j - 3 >= 0:
            pre_load(j - 3)
        yield
        ph, phr = pre_hT[j % 6]
        tr_part(hnb, hnr, ph[:], phr, copy_eng=("act" if j % 2 == 0 else "dve"))
        if j == NT:
            cp("pool", hhalo[:], ph[:, :, 0:1], phr, ["hhalo"])
        yield

    def prepass_back(j):
        s2 = j % 2
        s3 = j % 3
        ph, phr = pre_hT[j % 6]
        hr = phr[0]

        class _V:
            def __getitem__(self, key):
                return ph[key[0], key[1], :]
        hTs = _V()
        l1 = lrT1b[s3]
        l1r = "lrT1_%d" % s3
        spp, spr = [(t1[:, 0:256], "t1a"), (t1[:, 512:768], "t1b"), (eg[:, 0:256], "eg")][s3]
        ecp, ecr = [(ECtm[:, 0:256], "ECtm0"), (ECtm[:, 256:512], "ECtm1"), (rstdc[:, 0:256], "rstdc")][s3]
        khp, khr = [(khat[:, 0:256], "khat0"), (khat[:, 256:512], "khat1"), (AT[:, 0:256], "AT")][s3]
        vp, vr = [(Pn[:, 0:512], "Pn0"), (Pn[:, 512:1024], "Pn1"), (yg[:], "yg")][s3]
        dcp = dec[:, 2 * s3:2 * s3 + 2]
        dcr = "dec%d" % s3
        pl, plr = mbank()
        mm([(pl[0:16, 0:128], w_in[:, kc, 3088:3104], hTs[:, kc, 1:129], kc == 0, kc == 7) for kc in range(8)],
           ["w_in_lr"] + phr, [plr])
        cp("act", l1[0:16, :], pl[0:16, 0:128], [plr], [l1r])
        pv, pvr = mbank()
        mm([(pv[:], hTs[:, kc, 1:129], w_in[:, kc, 2048:2560], kc == 0, kc == 7) for kc in range(8)],
           ["w_in_kv"] + phr, [pvr])
        cp("act" if s2 == 1 else "dve", vp, pv[:], [pvr], [vr])
        yield
        pz, pzr = mbank()
        mm([(pz[:, 0:256], l1[0:17, :], wab1[0:17, :], True, True)], [l1r, "wab1"], [pzr])
        act(spp, pz[:, 0:256], AF.Exp, [pzr], [spr], scale=-1.0)
        act(spp, spp, AF.Ln, [], [spr], bias=1.0, scale=1.0)
        yield
        pc, pcr = mbank()
        mm([(pc[:, 0:256], tri4[:, 3, :], spp, True, True),
            (pc[:, 256:257], spp[:, 0:128], onec[:], True, True),
            (pc[:, 257:258], spp[:, 128:256], onec[:], True, True)], ["tri4", spr, "onec"], [pcr])
        act(ecp, pc[:, 0:256], AF.Exp, [pcr], [ecr], scale=-1.0 / 16.0)
        act(dcp, pc[:, 256:258], AF.Exp, [pcr], [dcr], scale=-1.0 / 16.0)
        pk, pkr = mbank()
        mm([(pk[:, 0:256], hTs[:, kc, 1:129], w_in[:, kc, 1792:2048], kc == 0, kc == 7) for kc in range(8)],
           ["w_in_kv"] + phr, [pkr])
        tt("dve", khp, pk[:, 0:256], ecp, ALU.mult, [pkr, ecr], [khr])
        yield
        if j < NT:
            cp("pool", Sstore[:, j, :], Spb[:], ["Spb"], ["Sst%d" % j])
        pu, pur = mbank()
        mm([(pu[(h % 2) * 64:(h % 2) * 64 + 64, (h // 2) * 128:(h // 2 + 1) * 128], khp[:, h * 64:(h + 1) * 64],
             vp[:, h * 128:(h + 1) * 128], True, True) for h in range(4)], [khr, vr], [pur])
        multi("dve", [(lambda e, p=p: e.scalar_tensor_tensor(out=Spb[:, p * 128:(p + 1) * 128], in0=Spb[:, p * 128:(p + 1) * 128],
                                                             scalar=dcp[:, p:p + 1], in1=pu[:, p * 128:(p + 1) * 128],
                                                             op0=ALU.mult, op1=ALU.add)) for p in range(2)],
              [pur, dcr], ["Spb"])
        yield

    def run_pipeline(gens, depth, period=2):
        active = []
        it = iter(gens)
        rnd = 0
        done = False
        while True:
            if period == 0:
                if not active:
                    while not done and len(active) < depth:
                        g = next(it, None)
                        if g is None:
                            done = True
                        else:
                            active.append(g)
            elif not done and len(active) < depth and rnd % period == 0:
                g = next(it, None)
                if g is None:
                    done = True
                else:
                    active.append(g)
            if done and not active:
                break
            for g in list(active):
                try:
                    next(g)
                except StopIteration:
                    active.remove(g)
            rnd += 1

    def stageA(i):
        s3 = i % 3
        xr = "xres%d" % s3
        dma("sync", xres[s3][:], D["x_own"][i * 128:(i + 1) * 128, :], "xr%d" % s3, writes=[xr])
        yield
        norm_part(xres[s3][:], xr, gA, "gA", hn[i % 2], ["hn%d" % (i % 2)])
        yield
        tr_part(hn[i % 2], ["hn%d" % (i % 2)], hT[s3][:, :, 1:129], ["hT%d" % s3])
        if i > 0:
            sp_ = (i - 1) % 3
            cp("pool", hT[s3][:, :, 0:1], hT[sp_][:, :, 128:129], ["hT%d" % sp_], ["hT%d" % s3])
            cp("pool", hT[sp_][:, :, 129:130], hT[s3][:, :, 1:2], ["hT%d" % s3], ["hT%d" % sp_])
        if i == NT - 1:
            cp("pool", hT[s3][:, :, 129:130], hhalo[:], ["hhalo"], ["hT%d" % s3])
        yield

    def stageB(i):
        s3 = i % 3
        hTs = hT[s3]
        hr = "hT%d" % s3
        xr = "xres%d" % s3
        def conv_ops(c):
            pcv, pcr = mbank()
            items = [(pcv[:, 0:128], w_in[:, kc, c * 128:(c + 1) * 128], hTs[:, kc, 1:129], kc == 0, kc == 7) for kc in range(8)]
            items += [(pcv[:, 128:258], w_in[:, kc, 512 + c * 128:512 + (c + 1) * 128], hTs[:, kc, 0:130], kc == 0, kc == 7) for kc in range(8)]
            items += [(pcv[:, 258:388], w_in[:, kc, 1024 + c * 128:1024 + (c + 1) * 128], hTs[:, kc, 0:130], kc == 0, kc == 7) for kc in range(8)]
            hc, yc = "h%d" % c, "y%d" % c
            return [
                lambda: mm(items, ["w_in_a", hr], [pcr]),
                lambda: cp("act", h_sb[:, c, :], pcv[:, 258:388], [pcr], [hc]),
                lambda: tt("dve", h_sb[:, c, :], pcv[:, 128:258], h_sb[:, c, :], ALU.mult, [pcr], [hc]),
                lambda: act(y_sb[:, c, :], h_sb[:, c, 0:128], AF.Copy, [hc, "cw"], [yc], scale=cw[:, 3 * c:3 * c + 1]),
                lambda: stt("dve", y_sb[:, c, :], h_sb[:, c, 1:129], cw[:, 3 * c + 1:3 * c + 2], y_sb[:, c, :], ALU.mult, ALU.add, [hc, "cw"], [yc]),
                lambda: stt("dve", y_sb[:, c, :], h_sb[:, c, 2:130], cw[:, 3 * c + 2:3 * c + 3], y_sb[:, c, :], ALU.mult, ALU.add, [hc, "cw"], [yc]),
                lambda: tt("dve", y_sb[:, c, :], y_sb[:, c, :], pcv[:, 0:128], ALU.mult, [pcr], [yc]),
                lambda: act(ysq[:, c, :], y_sb[:, c, :], AF.Square, [yc], ["ysq%d" % c]),
            ]

        def conv_chunk(c):
            for f in conv_ops(c):
                f()

        def conv_pair(c1, c2):
            a, b = conv_ops(c1), conv_ops(c2)
            for fa, fb in zip(a, b):
                fa()
                fb()

        pl, plr = mbank()
        mm([(pl[0:32, 0:128], w_in[:, kc, 3072:3104], hTs[:, kc, 1:129], kc == 0, kc == 7) for kc in range(8)],
           ["w_in_lr", hr], [plr])
        cp("act", lrT2[0:32, :], pl[0:32, 0:128], [plr], ["lrT2"])
        pv, pvr = mbank()
        mm([(pv[:], hTs[:, kc, 1:129], w_in[:, kc, 2048:2560], kc == 0, kc == 7) for kc in range(8)],
           ["w_in_kv", hr], [pvr])
        cp("act", v_sb[:], pv[:], [pvr], ["v_sb"])
        pg, pgr = mbank()
        mm([(pg[:], hTs[:, kc, 1:129], w_in[:, kc, 2560:3072], kc == 0, kc == 7) for kc in range(8)], ["w_in_g", hr], [pgr])
        act(eg[:], pg[:], AF.Exp, [pgr], ["eg"], scale=-1.0)
        act(eg[:], eg[:], AF.Ln, [], ["eg"], bias=1.0, scale=1.0)
        act(eg[:], eg[:], AF.Exp, [], ["eg"], scale=-1.0)
        tt("dve", eg[:], pg[:], eg[:], ALU.mult, [pgr], ["eg"])
        tt("pool", eg[:], eg[:], ggla[:], ALU.mult, ["ggla"], ["eg"])
        yield
        pz, pzr = mbank()
        mm([(pz[:], lrT2[0:33, :], wa2[0:33, :], True, True)], ["lrT2", "wa2"], [pzr])
        act(sp[:], pz[:], AF.Exp, [pzr], ["sp"], scale=-1.0)
        act(sp[:], sp[:], AF.Ln, [], ["sp"], bias=1.0, scale=1.0)

        yield
        pC, pCr = mbank()
        mm([(pC[:, 0:256], tri4[:, 2, :], sp[:, 0:256], True, True)], ["tri4", "sp"], [pCr])
        pB, pBr = mbank()
        mm([(pB[:, 0:128], sp[:, 0:128], tri4[:, 0, :], True, True),
            (pB[:, 128:256], sp[:, 128:256], tri4[:, 0, :], True, True),
            (pB[:, 256:384], sp[:, 256:384], tri4[:, 1, :], True, True),
            (pB[:, 384:512], sp[:, 384:512], tri4[:, 1, :], True, True)], ["tri4", "sp"], [pBr])
        act(Eneg[:], pB[:], AF.Exp, [pBr], ["Eneg"], scale=-1.0 / 16.0)
        act(Epos[:], pB[:], AF.Exp, [pBr], ["Epos"], scale=1.0 / 16.0)
        act(ECtm[:, 0:256], pC[:, 0:256], AF.Exp, [pCr], ["ECtm0"], scale=-1.0 / 16.0)
        yield
        pqk, pqr = mbank()
        mm([(pqk[:, c * 128:(c + 1) * 128], w_in[:, kc, 1536 + c * 128:1536 + (c + 1) * 128], hTs[:, kc, 1:129], kc == 0, kc == 7)
            for c in range(4) for kc in range(8)], ["w_in_a", "w_in_kv", hr], [pqr])
        def v3(ap2):
            return ap2.rearrange("p (c t) -> p c t", c=2)
        for kind, (src0, E_, e0, nm, isq) in enumerate(((0, Eneg, 0, "qt_pf", True), (0, Eneg, 256, "qt_pb", True),
                                                        (256, Epos, 0, "kt_pf", False), (256, Epos, 256, "kt_pb", False))):
            fns = []
            for b in range(2):
                rows = slice(b * 64, b * 64 + 64)
                o_ = qk4[rows, kind, b:4:2, :]
                i0 = v3(pqk[rows, src0:src0 + 256])
                i1 = v3(E_[rows, e0:e0 + 256])
                if isq:
                    fns.append(lambda e, o_=o_, i0=i0, i1=i1: e.scalar_tensor_tensor(out=o_, in0=i0, scalar=0.125, in1=i1,
                                                                                      op0=ALU.mult, op1=ALU.mult))
                else:
                    fns.append(lambda e, o_=o_, i0=i0, i1=i1: e.tensor_tensor(out=o_, in0=i0, in1=i1, op=ALU.mult))
            multi("dve", fns, [pqr, "Eneg" if isq else "Epos"], [nm])
        pk, pkr = mbank()
        mm([(pk[:, 0:256], hTs[:, kc, 1:129], w_in[:, kc, 1792:2048], kc == 0, kc == 7) for kc in range(8)],
           ["w_in_kv", hr], [pkr])
        tt("dve", khat[:, 0:256], pk[:, 0:256], ECtm[:, 0:256], ALU.mult, [pkr, "ECtm0"], ["khat0"])
        yield
        pAf, pAfr = mbank()
        pAb, pAbr = mbank()
        for (pa, par, qi, ki, qn, kn) in ((pAf, pAfr, 0, 2, "qt_pf", "kt_pf"), (pAb, pAbr, 1, 3, "qt_pb", "kt_pb")):
            mm([(pa[:, h * 128:(h + 1) * 128], qk4[:, ki, h, :], qk4[:, qi, h, :], True, True) for h in range(4)],
               [qn, kn], [par])
        t1v = t1[:, 0:512].rearrange("p (h t) -> p h t", h=4)
        t2v = t1[:, 512:1024].rearrange("p (h t) -> p h t", h=4)
        tt("dve", t1v, pAf[:].rearrange("p (h t) -> p h t", h=4), tri4[:, 0, :].unsqueeze(1).to_broadcast([128, 4, 128]),
           ALU.mult, [pAfr, "tri4"], ["t1a"])
        tt("dve", t2v, pAb[:].rearrange("p (h t) -> p h t", h=4), tri4[:, 2, :].unsqueeze(1).to_broadcast([128, 4, 128]),
           ALU.mult, [pAbr, "tri4"], ["t1b"])
        tt("pool", AT[:], t1[:, 0:512], t1[:, 512:1024], ALU.add, ["t1a", "t1b"], ["AT"])
        yield
        conv_chunk(0)
        yield
        cur, nxt = i % 2, (i + 1) % 2
        pO, pOr = mbank()
        items = []
        for h in range(4):
            b0 = (h % 2) * 64
            pr = h // 2
            items.append((pO[:, h * 128:(h + 1) * 128], AT[:, h * 128:(h + 1) * 128], v_sb[:, h * 128:(h + 1) * 128], True, False))
            items.append((pO[:, h * 128:(h + 1) * 128], qk4[:, 0, h, :], Spf_bf[cur][:, pr * 128:(pr + 1) * 128], False, False))
            items.append((pO[:, h * 128:(h + 1) * 128], qk4[:, 1, h, :], Sstore[:, i, pr * 128:(pr + 1) * 128], False, True))
        mm(items, ["AT", "v_sb", "qt_pf", "qt_pb", "Spfb%d" % cur, "Sst%d" % i], [pOr])
        pu, pur = mbank()
        mm([(pu[(h % 2) * 64:(h % 2) * 64 + 64, (h // 2) * 128:(h // 2 + 1) * 128], khat[:, h * 64:(h + 1) * 64],
             v_sb[:, h * 128:(h + 1) * 128], True, True) for h in range(4)], ["khat0", "v_sb"], [pur])
        multi("dve", [(lambda e, p=p: e.scalar_tensor_tensor(out=Spf[:, p * 128:(p + 1) * 128], in0=Spf[:, p * 128:(p + 1) * 128],
                                                             scalar=Eneg[:, p * 128 + 127:p * 128 + 128], in1=pu[:, p * 128:(p + 1) * 128],
                                                             op0=ALU.mult, op1=ALU.add)) for p in range(2)],
              [pur, "Eneg"], ["Spf"])
        cp("pool", Spf_bf[nxt][:], Spf[:], ["Spf"], ["Spfb%d" % nxt])
        k = statctr[0] % 16
        statctr[0] += 1
        rn = "stat%d" % k
        c4 = stat[:, 4 * k:4 * k + 4]
        acts([(junk[:, h * 128:(h + 1) * 128], pO[:, h * 128:(h + 1) * 128], AF.Square, dict(scale=float(128.0 ** -0.5), accum_out=c4[:, h:h + 1]))
              for h in range(4)], [pOr], ["junk", rn])
        act(c4, c4, AF.Ln, [], [rn], bias=EPS, scale=1.0)
        act(c4, c4, AF.Exp, [], [rn], scale=-0.5)
        multi("dve", [(lambda e, h=h: e.scalar_tensor_tensor(out=yg[:, h * 128:(h + 1) * 128], in0=pO[:, h * 128:(h + 1) * 128],
                                                             scalar=c4[:, h:h + 1], in1=gs[:, h * 128:(h + 1) * 128],
                                                             op0=ALU.mult, op1=ALU.mult)) for h in range(4)],
              [pOr, rn, "eg"], ["yg"])
        conv_chunk(1)
        yield
        conv_pair(2, 3)
        yield
        tb, tres = tbank()
        tr([(tb[:, h * 128:(h + 1) * 128], yg[:, h * 128:(h + 1) * 128]) for h in range(4)], ["yg"], [tres])
        cp("act", yT[:, 4:8, :], tb[:, 0:512].rearrange("p (k t) -> p k t", k=4), [tres], ["yTg"])
        yield
        pst, pstr = mbank()
        mm([(pst[:, c * 128:(c + 1) * 128], b64[:], ysq[:, c, :], True, True) for c in range(4)],
           ["b64"] + ["ysq%d" % c for c in range(4)], [pstr])
        act(rstdc[:], pst[:], AF.Ln, [pstr], ["rstdc"], bias=EPS, scale=1.0 / 64.0)
        act(rstdc[:], rstdc[:], AF.Exp, [], ["rstdc"], scale=-0.5)
        multi("dve", [(lambda e, c=c: e.scalar_tensor_tensor(out=yT[:, c, :], in0=y_sb[:, c, :], scalar=cnorm[:, c:c + 1],
                                                             in1=rstdc[:, c * 128:(c + 1) * 128], op0=ALU.mult, op1=ALU.mult))
                      for c in range(4)], ["y%d" % c for c in range(4)] + ["rstdc", "cnorm"], ["yTc"])
        yield
        for half in range(2):
            po, por = mbank()
            mm([(po[:], yT[:, c, :], w_out[:, c, half * 512:(half + 1) * 512], c == 0, c == 7) for c in range(8)],
               ["yTc", "yTg", "w_out"], [por])
            tt("dve", xres[s3][:, half * 512:(half + 1) * 512], xres[s3][:, half * 512:(half + 1) * 512], po[:], ALU.add,
               [por], [xr])
        yield

    def stageC(i):
        s3 = i % 3
        xr = "xres%d" % s3
        hs = (i + 1) % 2
        norm_part(xres[s3][:], xr, gB, "gB", hn[hs], ["hn%d" % hs])
        yield
        tr_part(hn[hs], ["hn%d" % hs], hT2[:], ["hT2"])
        yield
        for half in range(2):
            pq, pqr = mbank()
            mm([(pq[:, c * 128:(c + 1) * 128], w_xq[:, kc, (half * 4 + c) * 128:(half * 4 + c + 1) * 128], hT2[:, kc, :],
                 kc == 0, kc == 7) for c in range(4) for kc in range(8)], ["w_xq", "hT2"], [pqr])
            cp("act" if half == 0 else "dve", qxT[:, half * 4:half * 4 + 4, :], pq[:].rearrange("p (c t) -> p c t", c=4),
               [pqr], ["qxT%d" % half])
            if half == 0:
                yield
        pss = []
        for hp in range(2):
            ps_, psr = mbank()
            mm([(ps_[:, (h % 2) * 256:(h % 2) * 256 + 256], qxT[:, 2 * h + cc, :], KmT[:, 2 * h + cc, :], cc == 0, cc == 1)
                for h in (2 * hp, 2 * hp + 1) for cc in range(2)], ["qxT0", "qxT1", "KmT"], [psr])
            pss.append((ps_, psr))
        mx = sm4[:, 0:4]
        nmx = sm4[:, 4:8]
        sm = sm4[:, 8:12]
        rs = sm4[:, 12:16]
        multi("dve", [(lambda e, hp=hp: e.reduce_max(out=mx[:, 2 * hp:2 * hp + 2],
                                                     in_=pss[hp][0][:].rearrange("p (h m) -> p h m", h=2), axis=AX.X))
                      for hp in range(2)], [pss[0][1], pss[1][1]], ["mx"])
        ts("dve", nmx, mx, -1.0 / 16.0, None, ALU.mult, None, ["mx"], ["nmx"])
        Pf = t1[:].rearrange("p (h m) -> p h m", h=4)
        acts([(Pf[:, h, :], pss[h // 2][0][:, (h % 2) * 256:(h % 2) * 256 + 256], AF.Exp,
               dict(scale=1.0 / 16.0, bias=nmx[:, h:h + 1], accum_out=sm[:, h:h + 1])) for h in range(4)],
             [pss[0][1], pss[1][1], "nmx"], ["t1a", "t1b", "sm"])
        P.op("dve", lambda e: e.reciprocal(out=rs, in_=sm), ["sm"], ["rs"])
        multi("dve", [(lambda e, h=h: e.tensor_scalar(out=Pn[:, h * 256:(h + 1) * 256], in0=Pf[:, h, :], scalar1=rs[:, h:h + 1],
                                                      scalar2=None, op0=ALU.mult)) for h in (0, 1)],
              ["t1a", "t1b", "rs"], ["Pn0"])
        acts([(Pn[:, h * 256:(h + 1) * 256], Pf[:, h, :], AF.Copy, dict(scale=rs[:, h:h + 1])) for h in (2, 3)],
             ["t1a", "t1b", "rs"], ["Pn1"])
        yield
        yield
        tb, tres = tbank()
        tr([(tb[:, j * 128:(j + 1) * 128], Pn[:, j * 128:(j + 1) * 128]) for j in range(8)], ["Pn0", "Pn1"], [tres])
        cp("act", PT[:], tb[:].rearrange("p (k t) -> p k t", k=8), [tres], ["PT"])
        yield
        for half in range(2):
            pp, ppr = mbank()
            mm([(pp[:, c * 128:(c + 1) * 128], Vm[:, mc, (half * 4 + c) * 128:(half * 4 + c + 1) * 128],
                 PT[:, ((half * 4 + c) // 2) * 2 + mc, :], mc == 0, mc == 1) for c in range(4) for mc in range(2)],
               ["Vm", "PT"], [ppr])
            cp("act" if half == 0 else "dve", oT[:, half * 4:half * 4 + 4, :], pp[:].rearrange("p (c t) -> p c t", c=4),
               [ppr], ["oT%d" % half])
        yield
        for half in range(2):
            po, por = mbank()
            mm([(po[:], oT[:, c, :], w_xo[:, c, half * 512:(half + 1) * 512], c == 0, c == 7) for c in range(8)],
               ["oT0", "oT1", "w_xo"], [por])
            tt("dve", xres[s3][:, half * 512:(half + 1) * 512], xres[s3][:, half * 512:(half + 1) * 512], po[:], ALU.add,
               [por], [xr])
        dma("sync", x2s[i * 128:(i + 1) * 128, :], xres[s3][:], "xst%d" % s3, reads=[xr], writes=["x2s_%d" % i])
        yield

    def stageStore(i):
        s3 = i % 3
        dma("sync", x2s[i * 128:(i + 1) * 128, :], xres[s3][:], "xst%d" % s3, reads=["xres%d" % s3], writes=["x2s_%d" % i])
        yield

    def drive(gens):
        gens = [g for g in gens if g is not None]
        while gens:
            alive = []
            for g in gens:
                try:
                    next(g)
                    alive.append(g)
                except StopIteration:
                    pass
            gens = alive

    upto = UPTO
    if STOP >= 2:
        order = list(range(2 * NT - 1, -1, -1))
        batches = [order[k:k + 3] for k in range(0, len(order), 3)]
        drive([prepass_front(j) for j in batches[0]])
        for bi, bt in enumerate(batches):
            gens = [prepass_back(j) for j in bt]
            if bi + 1 < len(batches):
                gens += [prepass_front(j) for j in batches[bi + 1]]
            drive(gens)
            if bi == min(1, len(batches) - 1) and STOP >= 1:
                kv_compute()

    if upto >= 1 and STOP >= 3:
        def step(g, n=1):
            if g is None:
                return
            for _ in range(n):
                try:
                    next(g)
                except StopIteration:
                    return

        drive([stageA(0)])
        genB = stageB(0)
        step(genB, 3)
        genA = stageA(1) if NT > 1 else None
        step(genA)
        genC = None
        for i in range(0, NT + 1):
            if i >= NT:
                genB = None
            genCn = stageC(i) if i < NT else None
            genBn = stageB(i + 1) if i + 1 < NT else None
            genAn = stageA(i + 2) if i + 2 < NT else None
            step(genA); step(genB)
            step(genC)
            step(genA); step(genC); step(genB)
            step(genB); step(genC)
            step(genB); step(genC)
            step(genC); step(genB)
            step(genB); step(genC)
            step(genBn); step(genB)
            step(genC); step(genBn); step(genB)
            step(genCn)
            step(genBn)
            step(genAn)
            drive([genB, genC, genA])
            genB = genBn
            genA = genAn
            genC = genCn

    early_keys = ()
    if upto >= 2:
        AF_ = Alloc(nc, RBASE)
        w_up = AF_("w_up", [128, 8, 4096], BF16)
        assert AF_.off <= AM.offs["w_xq"]
        for b in range(8):
            dmas("pool", [(w_up[:, kc, b * 512:(b + 1) * 512], D["w_up"][kc * 128:(kc + 1) * 128, b * 512:(b + 1) * 512])
                          for kc in range(8)], "wu%d" % b, writes=["wu%d" % b] + ((WIN + ["w_out"]) if b == 0 else []), mode="group")
        early_keys = tuple("wu%d" % b for b in range(8))
    P.barrier(skip_keys=early_keys)

    if upto >= 2:
        w_dn = AF_("w_dn", [128, 32, 1024], BF16)
        xf = [AF_("xf%d" % i, [128, 2, 1024], F32) for i in range(2)]
        hnf = [AF_("hnf%d" % i, [128, 1024], BF16) for i in range(2)]
        hTf = [AF_("hTf%d" % i, [128, 8, 256], BF16) for i in range(2)]
        hff = AF_("hff", [128, 32, 256], BF16)
        rl = [AF_("rl%d" % i, [128, 512], F32) for i in range(2)]
        print("phase F sbuf end", AF_.off, "limit", SB_END)
        dma("sync", gA[:], D["g_mlp"].partition_broadcast(128), "c_gA", writes=["gA"])
        dma("sync", gB[:], D["g_fin"].partition_broadcast(128), "c_gB", writes=["gB"])
        for b in range(8):
            dmas("pool", [(w_dn[:, b * 4 + f, :], D["w_down"][(b * 4 + f) * 128:(b * 4 + f + 1) * 128, :]) for f in range(4)],
                 "wd%d" % b, writes=["wd%d" % b], mode="group")
        NST = NT // 2

        def F_load(j):
            s = j % 2
            src = x2s[j * 256:(j + 1) * 256, :].rearrange("(t p) d -> p t d", p=128)
            dma("sync", xf[s][:], src, "xfl%d" % s, reads=["x2s_%d" % (2 * j), "x2s_%d" % (2 * j + 1)], writes=["xf%d" % s])

        def F_norm(j):
            s = j % 2
            for t in range(2):
                norm_part(xf[s][:, t, :], "xf%d" % s, gA, "gA", hnf[t], ["hnf%d" % t])

        def F_tr(j):
            s = j % 2
            for t in range(2):
                tr_part(hnf[t], ["hnf%d" % t], hTf[s][:, :, t * 128:(t + 1) * 128], ["hTf%d" % s],
                        copy_eng=("act" if t == 0 else "dve"))

        def F_B(j, fps=range(16)):
            s = j % 2
            for fp in fps:
                pu, pur = mbank()
                mm([(pu[:, q * 256:(q + 1) * 256], w_up[:, kc, (2 * fp + q) * 128:(2 * fp + q + 1) * 128], hTf[s][:, kc, :],
                     kc == 0, kc == 7) for q in range(2) for kc in range(8)], ["wu%d" % (fp // 2), "hTf%d" % s], [pur])
                r = rl[fp % 2]
                rr = "rl%d" % (fp % 2)
                act(r[:], pu[:], AF.Relu, [pur], [rr])
                tt("dve" if fp % 2 == 0 else "pool", hff[:, 2 * fp:2 * fp + 2, :], r[:].rearrange("p (q t) -> p q t", q=2),
                   r[:].rearrange("p (q t) -> p q t", q=2), ALU.mult, [rr], ["hff%d" % fp])

        def F_C(j):
            s = j % 2
            xr = "xf%d" % s
            for t in range(2):
                for half in range(2):
                    pd, pdr = mbank()
                    mm([(pd[:], hff[:, fc, t * 128:(t + 1) * 128], w_dn[:, fc, half * 512:(half + 1) * 512], fc == 0, fc == 31)
                        for fc in range(32)], ["hff%d" % fp for fp in range(16)] + ["wd%d" % b for b in range(8)], [pdr])
                    tt("dve", xf[s][:, t, half * 512:(half + 1) * 512], xf[s][:, t, half * 512:(half + 1) * 512], pd[:], ALU.add,
                       [pdr], [xr])
            for t in range(2):
                c, rn = rms_stats(xf[s][:, t, :], xr, 1.0 / 32.0)
                stt("dve", xf[s][:, t, :], xf[s][:, t, :], c, gB[:], ALU.mult, ALU.mult, [rn, "gB"], [xr])
            dst = out[j * 256:(j + 1) * 256, :].rearrange("(t p) d -> p t d", p=128)
            dma("sync", dst, xf[s][:], "ost%d" % s, reads=[xr])

        F_load(0)
        F_norm(0)
        F_tr(0)
        for j in range(NST):
            if j + 1 < NST:
                F_load(j + 1)
            F_B(j, range(0, 6))
            if j + 1 < NST:
                F_norm(j + 1)
            F_B(j, range(6, 16))
            if j + 1 < NST:
                F_tr(j + 1)
            F_C(j)
    else:
        AF_ = Alloc(nc, RBASE)
        xf = [AF_("xf%d" % i, [128, 1024], F32) for i in range(2)]
        for i in range(NT):
            s = i % 2
            srcd = x2s if STOP >= 3 else D["x_own"]
            dma("sync", xf[s][:], srcd[i * 128:(i + 1) * 128, :], "xfl%d" % s, reads=["x2s_%d" % i], writes=["xf%d" % s])
            dma("sync", out[i * 128:(i + 1) * 128, :], xf[s][:], "ost%d" % s, reads=["xf%d" % s])

    P.barrier()
    sems = es.enter_context(contextlib.ExitStack())
    block = es.enter_context(nc.Block())
    P.emit(block, lambda name: sems.enter_context(nc.semaphore(name)))
    es.close()
    return nc


UPTO = 2
STOP = 9
STAGES = 'ABC'
BCUT = 99
PRE_DEPTH = 3
PRE_PERIOD = 0


def _consts():
    u = np.arange(128)[:, None]
    t = np.arange(128)[None, :]
    tri4 = np.concatenate([(u <= t), (u >= t), (u > t), (u < t)], axis=1).astype(np.float32)
    b64 = ((u // 64) == (t // 64)).astype(np.float32)
    return dict(ident=np.eye(128, dtype=np.float32), tri4=np.ascontiguousarray(tri4), b64=b64)


def make_in_maps(inp):
    f = lambda a: np.ascontiguousarray(np.asarray(a, dtype=np.float32))
    x = f(inp["x"]); mem = f(inp["mem"])
    w_in = f(inp["w_in"])[0]
    conv_w = f(inp["conv_w"])[0]
    consts = _consts()
    maps = []
    for core in range(8):
        b, half = core // 2, core % 2
        H = x.shape[1] // 2
        if half == 0:
            x_own = x[b, 0:H]; x_oth = x[b, H:2 * H]
            wi = w_in
            apf, bpf, apb, bpb = inp["w_af"][0], inp["b_af"][0], inp["w_ab"][0], inp["b_ab"][0]
            cwt = conv_w
        else:
            x_own = x[b, H:2 * H][::-1]; x_oth = x[b, 0:H][::-1]
            wi = np.concatenate([w_in[:, :3072], w_in[:, 3088:3104], w_in[:, 3072:3088]], axis=1)
            apf, bpf, apb, bpb = inp["w_ab"][0], inp["b_ab"][0], inp["w_af"][0], inp["b_af"][0]
            cwt = conv_w[::-1]
        wa2 = np.zeros((33, 512), np.float32)
        wa2[0:16, 0:256] = apf; wa2[16:32, 256:512] = apb; wa2[32, 0:256] = bpf; wa2[32, 256:512] = bpb
        wab1 = np.concatenate([f(apb), f(bpb)[None, :]], axis=0)
        cw = np.ascontiguousarray(f(cwt).T.reshape(4, 128, 3).transpose(1, 0, 2).reshape(128, 12))
        cnorm = np.ascontiguousarray(f(inp["conv_norm"])[0].reshape(4, 128).T)
        m = dict(x_own=f(x_own), x_oth=f(x_oth), mem=f(mem[b]), w_in=f(wi), w_out=f(inp["w_out"])[0], w_xq=f(inp["w_xq"])[0],
                 w_xkv=f(inp["w_xkv"])[0], w_xo=f(inp["w_xo"])[0], w_up=f(inp["w_up"])[0], w_down=f(inp["w_down"])[0],
                 g_mix=f(inp["mix_norm"])[0], g_xa=f(inp["xa_norm"])[0], g_mem=f(inp["mem_norm"])[0], g_mlp=f(inp["mlp_norm"])[0],
                 g_fin=f(inp["final_norm"]), g_gla4=np.ascontiguousarray(np.tile(f(inp["gla_norm"])[0], 4)),
                 cw=cw, cnorm=cnorm, wa2=wa2, wab1=f(wab1), **consts)
        maps.append(m)
    return maps


def assemble(results):
    H = NT * 128
    out = np.empty((4, 2 * H, 1024), np.float32)
    for core in range(len(results)):
        b, half = core // 2, core % 2
        o = np.asarray(results[core]["out"], dtype=np.float32)
        if half == 0:
            out[b, 0:H] = o
        else:
            out[b, H:2 * H] = o[::-1]
    return out


_NC_CACHE = {}


def kernel(**inputs):
    if "nc" not in _NC_CACHE:
        _NC_CACHE["nc"] = build_program()
    nc = _NC_CACHE["nc"]
    maps = make_in_maps(inputs)
    res = run_bass_kernel_spmd(nc, maps, core_ids=list(range(8)))
    return assemble(res.results)
```
